# Optimizing a Trainium2 kernel written in Bass

```python
import math
import jax
import jax.numpy as jnp
from jax import lax
import numpy as np

D_MODEL = 2048
BATCH = 16
SEQ = 2048
DEPTH = 2
DEC_BATCH = 32
DEC_SEQ = 32
PAST_LEN = 4096

CHUNK = 64
Q_BLOCK = 128
N_ATT_LAYERS = (DEPTH + 1) // 2
N_SSM_LAYERS = DEPTH // 2
H_FOX = 8
DH_FOX = D_MODEL // 16
W_FOX = H_FOX * DH_FOX
H_DIFF = 8
DK_DIFF = D_MODEL // 32
DV_DIFF = 2 * DK_DIFF
W_DIFF = H_DIFF * DV_DIFF
D_SSM = D_MODEL
GROUP_CH = 16
N_GROUPS = D_SSM // GROUP_CH
P_STATE = 64

EPS = 1e-6
NEG_INF = -1e30
FORGET_BIAS_INIT = 5.0
ALIBI_MAX_EXP = 8.0
ATT_SECTIONS = (W_FOX, W_FOX, W_FOX, H_FOX, W_FOX, H_DIFF * 2 * DK_DIFF, H_DIFF * 2 * DK_DIFF, W_DIFF, W_DIFF)
ATT_IN_COLS = 4 * W_FOX + H_FOX + 2 * H_DIFF * 2 * DK_DIFF + 2 * W_DIFF

kernel_name = 'hybrid_fox_diff_s5_stream_step'


def _rmsnorm(x, g):
    x32 = x.astype(jnp.float32)
    y = x32 * lax.rsqrt(jnp.mean(x32 * x32, axis=-1, keepdims=True) + EPS)
    return (y * g.astype(jnp.float32)).astype(x.dtype)


def _adaln(x, c, g, w_mod, b_mod):
    mod = jax.nn.silu(c) @ w_mod + b_mod
    shift, scale, gate = jnp.split(mod, 3, axis=-1)
    h = _rmsnorm(x, g) * (1.0 + scale[:, None, :]) + shift[:, None, :]
    return h.astype(x.dtype), gate[:, None, :]


def _sweep_query_blocks(fn, q_leaves, axes):
    L = q_leaves[0].shape[axes[0]]
    if L <= Q_BLOCK or L % Q_BLOCK != 0:
        return fn(*q_leaves)
    nb = L // Q_BLOCK
    blocked = tuple(
        jnp.moveaxis(a.reshape(a.shape[:ax] + (nb, Q_BLOCK) + a.shape[ax + 1:]), ax, 0)
        for a, ax in zip(q_leaves, axes))
    out = lax.map(lambda args: fn(*args), blocked)
    out = jnp.moveaxis(out, 0, 1)
    return out.reshape((out.shape[0], L) + out.shape[3:])


def _fox_attention(q, k_new, v_new, logf_new, k_past, v_past, logf_past):
    P = k_past.shape[1]
    T = q.shape[1]
    k = jnp.concatenate([k_past, k_new], axis=1)
    v = jnp.concatenate([v_past, v_new], axis=1)
    F = jnp.cumsum(jnp.concatenate([logf_past.astype(jnp.float32), logf_new.astype(jnp.float32)], axis=1), axis=1)
    Fk = jnp.transpose(F, (0, 2, 1))
    Fq = F[:, P:]
    k_pos = jnp.arange(P + T)
    q_pos = P + jnp.arange(T)
    scale = DH_FOX ** -0.5

    def block(qb, Fqb, qpb):
        s = jnp.einsum('bqhd,bshd->bhqs', qb, k).astype(jnp.float32) * scale
        s = s + jnp.transpose(Fqb, (0, 2, 1))[..., None] - Fk[:, :, None, :]
        s = jnp.where(k_pos[None, :] <= qpb[:, None], s, NEG_INF)
        p = jax.nn.softmax(s, axis=-1)
        return jnp.einsum('bhqs,bshd->bqhd', p.astype(v.dtype), v)

    return _sweep_query_blocks(block, (q, Fq, q_pos), (1, 1, 0))


def _diff_attention(q, k_new, v_new, k_past, v_past, lam, subln_g, lambda_init):
    P = k_past.shape[1]
    T = q.shape[1]
    k = jnp.concatenate([k_past, k_new], axis=1)
    v = jnp.concatenate([v_past, v_new], axis=1)
    k1 = k[..., :DK_DIFF]
    k2 = k[..., DK_DIFF:]
    k_pos = jnp.arange(P + T)
    q_pos = P + jnp.arange(T)
    k_chunk = k_pos // CHUNK
    slopes = jnp.power(2.0, -ALIBI_MAX_EXP * jnp.arange(1, H_DIFF + 1, dtype=jnp.float32) / H_DIFF)
    scale = DK_DIFF ** -0.5

    def block(qb, qpb):
        dist = jnp.abs(qpb[:, None] - k_pos[None, :]).astype(jnp.float32)
        bias = -slopes[:, None, None] * dist[None]
        visible = k_chunk[None, :] <= (qpb // CHUNK)[:, None]

        def probs(qh, kh):
            s = jnp.einsum('bqhd,bshd->bhqs', qh, kh).astype(jnp.float32) * scale + bias
            s = jnp.where(visible, s, NEG_INF)
            return jax.nn.softmax(s, axis=-1)

        p = probs(qb[..., :DK_DIFF], k1) - lam * probs(qb[..., DK_DIFF:], k2)
        o = jnp.einsum('bhqs,bshd->bqhd', p.astype(v.dtype), v)
        return _rmsnorm(o, subln_g) * (1.0 - lambda_init)

    return _sweep_query_blocks(block, (q, q_pos), (1, 0))


def _attention_layer(x, c, l, ia, fk_past, fv_past, fl_past, dk_past, dv_past, p):
    B, T, _ = x.shape
    h, gate = _adaln(x, c, p['norm_g'][l], p['w_mod'][l], p['b_mod'][l])
    proj = h @ p['w_in_att'][ia]
    points = np.cumsum(ATT_SECTIONS)[:-1].tolist()
    q_f, k_f, v_f, f_logit, z_f, q_d, k_d, v_d, z_d = jnp.split(proj, points, axis=-1)
    logf = jax.nn.log_sigmoid(f_logit.astype(jnp.float32) + p['b_forget'][ia])
    q_f = q_f.reshape(B, T, H_FOX, DH_FOX)
    k_f = k_f.reshape(B, T, H_FOX, DH_FOX)
    v_f = v_f.reshape(B, T, H_FOX, DH_FOX)
    q_d = q_d.reshape(B, T, H_DIFF, 2 * DK_DIFF)
    k_d = k_d.reshape(B, T, H_DIFF, 2 * DK_DIFF)
    v_d = v_d.reshape(B, T, H_DIFF, DV_DIFF)
    lambda_init = 0.8 - 0.6 * math.exp(-0.3 * l)
    lam = (jnp.exp(jnp.sum(p['diff_lq1'][ia] * p['diff_lk1'][ia]))
           - jnp.exp(jnp.sum(p['diff_lq2'][ia] * p['diff_lk2'][ia])) + lambda_init).astype(jnp.float32)
    o_f = _fox_attention(q_f, k_f, v_f, logf, fk_past, fv_past, fl_past).reshape(B, T, W_FOX)
    o_d = _diff_attention(q_d, k_d, v_d, dk_past, dv_past, lam, p['diff_subln_g'][ia], lambda_init).reshape(B, T, W_DIFF)
    mixed = jnp.concatenate([o_f * jax.nn.silu(z_f), o_d * jax.nn.silu(z_d)], axis=-1)
    x = x + gate * (mixed @ p['w_out_att'][ia])
    return x, (k_f, v_f, logf, k_d, v_d)


def _s5_discretise(a_re, a_im, b_re, b_im, log_dt):
    dt = jnp.exp(log_dt.astype(jnp.float32))[:, None]
    mag = jnp.exp(dt * a_re)
    abar_re = mag * jnp.cos(dt * a_im)
    abar_im = mag * jnp.sin(dt * a_im)
    den = a_re * a_re + a_im * a_im
    nr = abar_re - 1.0
    ni = abar_im
    fr = (nr * a_re + ni * a_im) / den
    fi = (ni * a_re - nr * a_im) / den
    bb_re = fr[..., None] * b_re - fi[..., None] * b_im
    bb_im = fr[..., None] * b_im + fi[..., None] * b_re
    return abar_re, abar_im, bb_re, bb_im


def _s5_chunk(carry, u, abar_re, abar_im, bb_re, bb_im, c_re, c_im):
    T = u.shape[1]
    bu_re = jnp.einsum('btgc,gpc->btgp', u, bb_re)
    bu_im = jnp.einsum('btgc,gpc->btgp', u, bb_im)
    a_re = jnp.broadcast_to(abar_re[None, None], (1, T) + abar_re.shape)
    a_im = jnp.broadcast_to(abar_im[None, None], (1, T) + abar_im.shape)

    def combine(e_i, e_j):
        ar_i, ai_i, br_i, bi_i = e_i
        ar_j, ai_j, br_j, bi_j = e_j
        return (ar_j * ar_i - ai_j * ai_i,
                ar_j * ai_i + ai_j * ar_i,
                ar_j * br_i - ai_j * bi_i + br_j,
                ar_j * bi_i + ai_j * br_i + bi_j)

    ac_re, ac_im, x_re, x_im = lax.associative_scan(combine, (a_re, a_im, bu_re, bu_im), axis=1)
    cr, ci = carry
    x_re = x_re + ac_re * cr[:, None] - ac_im * ci[:, None]
    x_im = x_im + ac_re * ci[:, None] + ac_im * cr[:, None]
    y = jnp.einsum('btgp,gcp->btgc', x_re, c_re) - jnp.einsum('btgp,gcp->btgc', x_im, c_im)
    return (x_re[:, -1], x_im[:, -1]), y


def _s5_sequence(u, carry, abar_re, abar_im, bb_re, bb_im, c_re, c_im):
    B, L = u.shape[:2]
    if L > CHUNK and L % CHUNK == 0:
        nc = L // CHUNK
        uc = jnp.transpose(u.reshape(B, nc, CHUNK, N_GROUPS, GROUP_CH), (1, 0, 2, 3, 4))

        def step(cr, ub):
            return _s5_chunk(cr, ub, abar_re, abar_im, bb_re, bb_im, c_re, c_im)

        carry, ys = lax.scan(step, carry, uc)
        y = jnp.transpose(ys, (1, 0, 2, 3, 4)).reshape(B, L, N_GROUPS, GROUP_CH)
        return y, carry
    carry, y = _s5_chunk(carry, u, abar_re, abar_im, bb_re, bb_im, c_re, c_im)
    return y, carry


def _ssm_layer(x, c, l, isd, s_re, s_im, p):
    B, T, _ = x.shape
    h, gate = _adaln(x, c, p['norm_g'][l], p['w_mod'][l], p['b_mod'][l])
    u, z = jnp.split(h @ p['w_in_ssm'][isd], 2, axis=-1)
    u32 = u.astype(jnp.float32)
    abar_re, abar_im, bb_re, bb_im = _s5_discretise(p['ssm_a_re'][isd], p['ssm_a_im'][isd],
                                                    p['ssm_b_re'][isd], p['ssm_b_im'][isd], p['ssm_log_dt'][isd])
    y, (n_re, n_im) = _s5_sequence(u32.reshape(B, T, N_GROUPS, GROUP_CH),
                                   (s_re.astype(jnp.float32), s_im.astype(jnp.float32)),
                                   abar_re, abar_im, bb_re, bb_im, p['ssm_c_re'][isd], p['ssm_c_im'][isd])
    y = y.reshape(B, T, D_SSM) + p['ssm_d'][isd] * u32
    g = jax.nn.gelu(y)
    glu = g * jax.nn.sigmoid(g @ p['w_glu'][isd] + p['b_glu'][isd])
    out = (glu.astype(x.dtype) * jax.nn.silu(z)) @ p['w_out_ssm'][isd]
    x = x + gate * out
    return x, (n_re, n_im)


def _trunk(x, c, fk, fv, fl, dk, dv, s_re, s_im, p):
    att_rows = []
    ssm_states = []
    for l in range(DEPTH):
        if l % 2 == 0:
            ia = l // 2
            x, rows = _attention_layer(x, c, l, ia, fk[ia], fv[ia], fl[ia], dk[ia], dv[ia], p)
            att_rows.append(rows)
        else:
            isd = l // 2
            x, st = _ssm_layer(x, c, l, isd, s_re[isd], s_im[isd], p)
            ssm_states.append(st)
    y = _rmsnorm(x, p['final_norm_g'])
    att_out = tuple(jnp.stack([r[i] for r in att_rows]) for i in range(5))
    ssm_out = tuple(jnp.stack([s[i] for s in ssm_states]) for i in range(2))
    return y, att_out, ssm_out


def setup_inputs(seed: int = 0) -> dict:
    key = jax.random.key(seed)
    ks = jax.random.split(key, 40)
    f32 = jnp.float32

    def nrm(k, shape, s=1.0):
        return s * jax.random.normal(k, shape, f32)

    att_cache = (N_ATT_LAYERS, DEC_BATCH, PAST_LEN)
    ssm_state = (N_SSM_LAYERS, DEC_BATCH, N_GROUPS, P_STATE)
    ssm_p = (N_SSM_LAYERS, N_GROUPS, P_STATE)
    return {
        'x_prompt': nrm(ks[0], (BATCH, SEQ, D_MODEL)),
        'x_sample': nrm(ks[1], (DEC_BATCH, DEC_SEQ, D_MODEL)),
        'cache_fox_k': nrm(ks[2], att_cache + (H_FOX, DH_FOX)),
        'cache_fox_v': nrm(ks[3], att_cache + (H_FOX, DH_FOX)),
        'cache_fox_logf': jax.nn.log_sigmoid(FORGET_BIAS_INIT + nrm(ks[4], att_cache + (H_FOX,))),
        'cache_diff_k': nrm(ks[5], att_cache + (H_DIFF, 2 * DK_DIFF)),
        'cache_diff_v': nrm(ks[6], att_cache + (H_DIFF, DV_DIFF)),
        'state_ssm_re': nrm(ks[7], ssm_state, 0.3),
        'state_ssm_im': nrm(ks[8], ssm_state, 0.3),
        'c_prompt': nrm(ks[9], (BATCH, D_MODEL)),
        'c_sample': nrm(ks[10], (DEC_BATCH, D_MODEL)),
        'norm_g': 1.0 + nrm(ks[11], (DEPTH, D_MODEL), 0.01),
        'w_mod': nrm(ks[12], (DEPTH, D_MODEL, 3 * D_MODEL), D_MODEL ** -0.5),
        'b_mod': nrm(ks[13], (DEPTH, 3 * D_MODEL), 0.01),
        'w_in_att': nrm(ks[14], (N_ATT_LAYERS, D_MODEL, ATT_IN_COLS), D_MODEL ** -0.5),
        'b_forget': FORGET_BIAS_INIT + nrm(ks[15], (N_ATT_LAYERS, H_FOX), 0.5),
        'w_out_att': nrm(ks[16], (N_ATT_LAYERS, W_FOX + W_DIFF, D_MODEL), (W_FOX + W_DIFF) ** -0.5),
        'diff_lq1': nrm(ks[17], (N_ATT_LAYERS, DK_DIFF), 0.1),
        'diff_lk1': nrm(ks[18], (N_ATT_LAYERS, DK_DIFF), 0.1),
        'diff_lq2': nrm(ks[19], (N_ATT_LAYERS, DK_DIFF), 0.1),
        'diff_lk2': nrm(ks[20], (N_ATT_LAYERS, DK_DIFF), 0.1),
        'diff_subln_g': 1.0 + nrm(ks[21], (N_ATT_LAYERS, DV_DIFF), 0.01),
        'w_in_ssm': nrm(ks[22], (N_SSM_LAYERS, D_MODEL, 2 * D_SSM), D_MODEL ** -0.5),
        'ssm_a_re': -0.5 + nrm(ks[23], ssm_p, 0.01),
        'ssm_a_im': math.pi * jnp.arange(P_STATE, dtype=f32) + nrm(ks[24], ssm_p, 0.01),
        'ssm_b_re': nrm(ks[25], ssm_p + (GROUP_CH,), (2 * GROUP_CH) ** -0.5),
        'ssm_b_im': nrm(ks[26], ssm_p + (GROUP_CH,), (2 * GROUP_CH) ** -0.5),
        'ssm_c_re': nrm(ks[27], (N_SSM_LAYERS, N_GROUPS, GROUP_CH, P_STATE), P_STATE ** -0.5),
        'ssm_c_im': nrm(ks[28], (N_SSM_LAYERS, N_GROUPS, GROUP_CH, P_STATE), P_STATE ** -0.5),
        'ssm_d': nrm(ks[29], (N_SSM_LAYERS, D_SSM)),
        'ssm_log_dt': jax.random.uniform(ks[30], (N_SSM_LAYERS, N_GROUPS), f32, math.log(0.001), math.log(0.1)),
        'w_glu': nrm(ks[31], (N_SSM_LAYERS, D_SSM, D_SSM), D_SSM ** -0.5),
        'b_glu': nrm(ks[32], (N_SSM_LAYERS, D_SSM), 0.01),
        'w_out_ssm': nrm(ks[33], (N_SSM_LAYERS, D_SSM, D_MODEL), D_SSM ** -0.5),
        'final_norm_g': 1.0 + nrm(ks[34], (D_MODEL,), 0.01),
    }


def reference(x_prompt, x_sample, cache_fox_k, cache_fox_v, cache_fox_logf, cache_diff_k, cache_diff_v,
              state_ssm_re, state_ssm_im, c_prompt, c_sample, norm_g, w_mod, b_mod, w_in_att, b_forget,
              w_out_att, diff_lq1, diff_lk1, diff_lq2, diff_lk2, diff_subln_g, w_in_ssm, ssm_a_re, ssm_a_im,
              ssm_b_re, ssm_b_im, ssm_c_re, ssm_c_im, ssm_d, ssm_log_dt, w_glu, b_glu, w_out_ssm, final_norm_g):
    p = dict(norm_g=norm_g, w_mod=w_mod, b_mod=b_mod, w_in_att=w_in_att, b_forget=b_forget,
             w_out_att=w_out_att, diff_lq1=diff_lq1, diff_lk1=diff_lk1, diff_lq2=diff_lq2, diff_lk2=diff_lk2,
             diff_subln_g=diff_subln_g, w_in_ssm=w_in_ssm, ssm_a_re=ssm_a_re, ssm_a_im=ssm_a_im,
             ssm_b_re=ssm_b_re, ssm_b_im=ssm_b_im, ssm_c_re=ssm_c_re, ssm_c_im=ssm_c_im, ssm_d=ssm_d,
             ssm_log_dt=ssm_log_dt, w_glu=w_glu, b_glu=b_glu, w_out_ssm=w_out_ssm, final_norm_g=final_norm_g)
    B = x_prompt.shape[0]
    dt = x_prompt.dtype
    e_fk = jnp.zeros((N_ATT_LAYERS, B, 0, H_FOX, DH_FOX), dt)
    e_fl = jnp.zeros((N_ATT_LAYERS, B, 0, H_FOX), jnp.float32)
    e_dk = jnp.zeros((N_ATT_LAYERS, B, 0, H_DIFF, 2 * DK_DIFF), dt)
    e_dv = jnp.zeros((N_ATT_LAYERS, B, 0, H_DIFF, DV_DIFF), dt)
    z_s = jnp.zeros((N_SSM_LAYERS, B, N_GROUPS, P_STATE), jnp.float32)
    y_prompt, (fk_p, fv_p, fl_p, dk_p, dv_p), (sre_p, sim_p) = _trunk(
        x_prompt, c_prompt, e_fk, e_fk, e_fl, e_dk, e_dv, z_s, z_s, p)
    y_sample, (fk_s, fv_s, fl_s, dk_s, dv_s), (sre_s, sim_s) = _trunk(
        x_sample, c_sample, cache_fox_k, cache_fox_v, cache_fox_logf, cache_diff_k, cache_diff_v,
        state_ssm_re, state_ssm_im, p)
    return (y_prompt, y_sample, fk_p, fv_p, fl_p, dk_p, dv_p, sre_p, sim_p,
            fk_s, fv_s, fl_s, dk_s, dv_s, sre_s, sim_s)
```

```python
import numpy as np
from contextlib import ExitStack
import concourse.bass as bass
import concourse.mybir as mybir
from concourse.bass_utils import run_bass_kernel_spmd

F32 = mybir.dt.float32
BF16 = mybir.dt.bfloat16
AF = mybir.ActivationFunctionType
ALU = mybir.AluOpType

NDMA_SLOTS = 12
EPOCH = 16000
NCORES = 8
D = 2048
EPS = 1e-6
COLS_ATT = 8200


class Buf:
    __slots__ = ("w", "r", "excl")

    def __init__(self, excl=False):
        self.w = None
        self.r = {}
        self.excl = excl


class Prog:
    ENGS = ("pe", "act", "dve", "pool", "sp")

    def __init__(self, nc):
        self.nc = nc
        self.streams = {e: [] for e in self.ENGS}
        self.cnt = {e: 0 for e in self.ENGS}
        self.seen = {e: {} for e in self.ENGS}
        self.ndma = 0
        self.dma_last = {}
        self.es = ExitStack()
        self.sems = {}
        self.pending = {}

    def sb(self, name, shape, dt):
        return self.es.enter_context(self.nc.sbuf_tensor(name, list(shape), dt))

    def ps(self, name, shape, dt):
        return self.es.enter_context(self.nc.psum_tensor(name, list(shape), dt))

    def _collect(self, eng, reads, writes):
        waits = {}

        def need(kv):
            k, v = kv
            if v > waits.get(k, 0):
                waits[k] = v
        for b in reads:
            if b.w is not None:
                need(b.w)
            if b.excl:
                for kv in b.r.items():
                    need(kv)
        for b in writes:
            if b.w is not None:
                need(b.w)
            for kv in b.r.items():
                need(kv)
        out = []
        seen = self.seen[eng]
        for k, v in waits.items():
            if k[0] == "pe" and eng == "pe":
                continue
            if seen.get(k, 0) >= v:
                continue
            seen[k] = v
            out.append((k, v))
        return out

    def _mark(self, key, val, reads, writes):
        for b in reads:
            if b.r.get(key, 0) < val:
                b.r[key] = val
        for b in writes:
            b.w = (key, val)
            b.r = {}

    def barrier(self):
        snap = {}
        for e in self.ENGS[:4]:
            c = self.cnt[e]
            if c > 0:
                ep = (c - 1) // EPOCH
                snap[(e, ep)] = c - ep * EPOCH
        for s_, v in self.dma_last.items():
            snap[("dma", s_)] = v
        for e in self.ENGS:
            self.pending[e] = dict(snap)

    def _pend(self, eng, waits):
        pend = self.pending.get(eng)
        if pend:
            seen = self.seen[eng]
            have = dict(waits)
            for k, v in pend.items():
                if k[0] == "pe" and eng == "pe":
                    continue
                if seen.get(k, 0) >= v or have.get(k, 0) >= v:
                    continue
                seen[k] = v
                waits.append((k, v))
            self.pending[eng] = None
        return waits

    def phase(self):
        prog = self

        class _Ph:
            def __enter__(self_):
                self_.saved = prog.es
                prog.es = ExitStack()
                return self_

            def __exit__(self_, *a):
                prog.barrier()
                prog.es.close()
                prog.es = self_.saved
                return False
        return _Ph()

    def op(self, eng, fn, reads=(), writes=()):
        waits = self._pend(eng, self._collect(eng, reads, writes))
        self.cnt[eng] += 1
        c = self.cnt[eng]
        ep = (c - 1) // EPOCH
        key = (eng, ep)
        self.streams[eng].append((waits, fn, (key, 1)))
        self._mark(key, c - ep * EPOCH, reads, writes)

    def dma(self, fn, reads=(), writes=(), eng="sp"):
        waits = self._pend(eng, self._collect(eng, reads, writes))
        slot = self.ndma % NDMA_SLOTS
        self.ndma += 1
        key = ("dma", slot)
        prev = self.dma_last.get(slot, 0)
        if prev and self.seen[eng].get(key, 0) < prev:
            self.seen[eng][key] = prev
            waits.append((key, prev))
        val = prev + 16
        self.dma_last[slot] = val
        self.streams[eng].append((waits, fn, (key, 16)))
        self._mark(key, val, reads, writes)

    def emit(self):
        nc = self.nc
        with ExitStack() as es:
            keys = [("dma", s) for s in range(NDMA_SLOTS)]
            for e in self.ENGS[:4]:
                for ep in range((max(self.cnt[e], 1) - 1) // EPOCH + 1):
                    keys.append((e, ep))
            for k in keys:
                self.sems[k] = es.enter_context(nc.semaphore("s_%s%d" % (k[0], k[1])))
            fin = [(("dma", s), v) for s, v in self.dma_last.items()]
            engmap = {"pe": "tensor", "act": "scalar", "dve": "vector", "pool": "gpsimd", "sp": "sync"}
            with nc.Block() as block:
                for e in self.ENGS:
                    stream = self.streams[e]
                    final = fin if e == "sp" else []

                    def body(engine, stream=stream, final=final):
                        sems = self.sems
                        for waits, fn, (k, inc) in stream:
                            for wk, wv in waits:
                                engine.wait_ge(sems[wk], wv)
                            fn(engine).then_inc(sems[k], inc)
                        for wk, wv in final:
                            engine.wait_ge(sems[wk], wv)
                    getattr(block, engmap[e])(body)
        self.es.close()


IN_SPECS = [
    ("xp", [4096, D]), ("xs", [128, D]),
    ("cfk", [4, 4096, 1024]), ("cfv", [4, 4096, 1024]), ("cfl", [4, 4096, 8]),
    ("cdk", [4, 4096, 1024]), ("cdv", [4, 4096, 1024]),
    ("sre", [4, 128, 64]), ("sim", [4, 128, 64]), ("cc", [6, D]),
    ("norm_g", [2, D]), ("w_mod", [2, D, 3 * D]), ("b_mod", [2, 3 * D]),
    ("w_in_att", [D, COLS_ATT]), ("b_forget", [1, 8]), ("w_out_att", [D, D]),
    ("lq1", [1, 64]), ("lk1", [1, 64]), ("lq2", [1, 64]), ("lk2", [1, 64]), ("subln_g", [1, 128]),
    ("w_in_ssm", [D, 2 * D]), ("a_re", [128, 64]), ("a_im", [128, 64]),
    ("b_re", [128, 64, 16]), ("b_im", [128, 64, 16]), ("c_re", [128, 16, 64]), ("c_im", [128, 16, 64]),
    ("ssm_d", [1, D]), ("log_dt", [1, 128]), ("w_glu", [D, D]), ("b_glu", [1, D]), ("w_out_ssm", [D, D]),
    ("final_g", [1, D]),
]
OUT_SPECS = [
    ("yp", [4096, D]), ("ys", [128, D]),
    ("fkp", [4096, 1024]), ("fvp", [4096, 1024]), ("flp", [4096, 8]), ("dkp", [4096, 1024]), ("dvp", [4096, 1024]),
    ("srep", [2, 128, 64]), ("simp", [2, 128, 64]),
    ("fks", [128, 1024]), ("fvs", [128, 1024]), ("fls", [128, 8]), ("dks", [128, 1024]), ("dvs", [128, 1024]),
    ("sres", [4, 128, 64]), ("sims", [4, 128, 64]),
]


def build():
    import os
    STOP = os.environ.get('KSTOP', '')

    def finish():
        P.emit()
        return nc
    nc = bass.Bass("TRN2", target_bir_lowering=False, dynamic_dma_scratch_size=1024)
    I = {n: nc.dram_tensor(n, s, F32, kind="ExternalInput").ap() for n, s in IN_SPECS}
    O = {n: nc.dram_tensor(n, s, F32, kind="ExternalOutput").ap() for n, s in OUT_SPECS}
    P = Prog(nc)

    def dram(name, shape, dt):
        return nc.dram_tensor(name, list(shape), dt, kind="Internal").ap()

    WiaD = dram("WiaD", [D, COLS_ATT], BF16)
    b_WiaD = Buf()
    modD = dram("modD", [2, 6, 3 * D], F32)
    b_modD = Buf()

    identB = P.sb("identB", [128, 128], BF16)
    identF = P.sb("identF", [128, 128], F32)
    b_const = Buf()
    for t in (identB, identF):
        P.op("pool", lambda e, t=t: e.memset(t[:], 1.0), writes=[b_const])
        P.op("pool", lambda e, t=t: e.affine_select(out=t[:], in_=t[:], pattern=[[-1, 128]], compare_op=ALU.is_equal,
                                                     fill=0.0, base=0, channel_multiplier=1),
             reads=[b_const], writes=[b_const])

    stgF = [P.sb("stgF%d" % i, [128, D], F32) for i in range(2)]
    b_stgF = [Buf(), Buf()]
    stgB = [P.sb("stgB%d" % i, [128, D], BF16) for i in range(2)]
    b_stgB = [Buf(), Buf()]
    psum = [P.ps("ps%d" % i, [128, 512], F32) for i in range(6)]
    b_ps = [Buf(True) for _ in range(8)]

    cast_rr = [0]

    def cast(out, in_, reads, writes):
        i = cast_rr[0] % 3
        cast_rr[0] += 1
        if i == 0:
            P.op("act", lambda e: e.activation(out=out, in_=in_, func=AF.Copy), reads, writes)
        elif i == 1:
            P.op("dve", lambda e: e.tensor_copy(out=out, in_=in_), reads, writes)
        else:
            P.op("pool", lambda e: e.tensor_copy(out=out, in_=in_), reads, writes)

    def conv_weight(src, dst, b_dst, ncols):
        i = 0
        for kc in range(16):
            for c0 in range(0, ncols, D):
                cw = min(D, ncols - c0)
                s = i % 2
                i += 1
                P.dma(lambda e, s=s, kc=kc, c0=c0, cw=cw: e.dma_start(out=stgF[s][:, 0:cw], in_=src[kc * 128:(kc + 1) * 128, c0:c0 + cw]),
                      writes=[b_stgF[s]])
                cast(stgB[s][:, 0:cw], stgF[s][:, 0:cw], [b_stgF[s]], [b_stgB[s]])
                P.dma(lambda e, s=s, kc=kc, c0=c0, cw=cw: e.dma_start(out=dst[kc * 128:(kc + 1) * 128, c0:c0 + cw], in_=stgB[s][:, 0:cw]),
                      reads=[b_stgB[s]], writes=[b_dst])

    conv_weight(I["w_in_att"], WiaD, b_WiaD, COLS_ATT)

    b_c6 = Buf()
    scT = P.sb("scT", [128, 16, 6], F32)
    b_scT = Buf()
    hT = P.sb("hT", [128, 16, D], BF16)
    b_hT = [Buf() for _ in range(16)]
    arenaF = hT[:].rearrange("p k t -> p (k t)").bitcast(F32)
    mod_sb = arenaF[0:6, 0:3 * D]
    b_mod = Buf()
    bmod = arenaF[0:6, 3 * D:6 * D]
    b_bmod = Buf()
    c6 = arenaF[0:6, 6 * D:7 * D]
    modT = [P.sb("modT%d" % l, [128, 48, 6], F32) for l in range(2)]
    b_modT = [Buf(), Buf()]
    Acol = [P.sb("Acol%d" % l, [128, 16, 6], F32) for l in range(2)]
    b_A = [Buf(), Buf()]
    g16 = P.sb("g16", [16, 128], F32)
    b_g16 = Buf()
    gT = P.sb("gT", [128, 16], F32)
    b_gT = Buf()

    P.dma(lambda e: e.dma_start(out=c6, in_=I["cc"]), writes=[b_c6])
    P.op("act", lambda e: e.activation(out=c6, in_=c6, func=AF.Silu), reads=[b_c6], writes=[b_c6])
    for k in range(16):
        P.op("pe", lambda e, k=k: e.transpose(out=psum[0][:, k * 6:(k + 1) * 6], in_=c6[0:6, k * 128:(k + 1) * 128],
                                              identity=identF[0:6, 0:6]), reads=[b_c6, b_const], writes=[b_ps[0]])
    P.op("dve", lambda e: e.tensor_copy(out=scT[:].rearrange("p k b -> p (k b)"), in_=psum[0][:, 0:96]),
         reads=[b_ps[0]], writes=[b_scT])
    for l in range(2):
        P.dma(lambda e, l=l: e.dma_start(out=bmod, in_=I["b_mod"][l:l + 1, :].to_broadcast([6, 3 * D])), writes=[b_bmod])
        for half in range(2):
            for k in range(16):
                s = k % 2
                for q in range(2):
                    P.dma(lambda e, l=l, half=half, k=k, s=s, q=q: e.dma_start(
                        out=(stgF[s][:, 0:2048] if q == 0 else stgB[s][:].bitcast(F32)[:, 0:1024]),
                        in_=I["w_mod"][l, k * 128:(k + 1) * 128, half * 3072 + q * 2048: half * 3072 + (2048 if q == 0 else 3072)]),
                        writes=[b_stgF[s] if q == 0 else b_stgB[s]])
                for nb in range(6):
                    if nb < 4:
                        rhs = stgF[s][:, nb * 512:(nb + 1) * 512]
                        rb = b_stgF[s]
                    else:
                        rhs = stgB[s][:].bitcast(F32)[:, (nb - 4) * 512:(nb - 3) * 512]
                        rb = b_stgB[s]
                    P.op("pe", lambda e, k=k, nb=nb, rhs=rhs: e.matmul(psum[nb][0:6, :], lhsT=scT[:, k, :], rhs=rhs,
                                                                        start=(k == 0), stop=(k == 15)),
                         reads=[b_scT, rb], writes=[b_ps[nb]])
            for nb in range(6):
                c0 = half * 3072 + nb * 512
                P.op("dve", lambda e, nb=nb, c0=c0: e.tensor_tensor(out=mod_sb[:, c0:c0 + 512], in0=psum[nb][0:6, :],
                                                                    in1=bmod[:, c0:c0 + 512], op=ALU.add),
                     reads=[b_ps[nb], b_bmod], writes=[b_mod])
        P.dma(lambda e, l=l: e.dma_start(out=modD[l], in_=mod_sb), reads=[b_mod], writes=[b_modD])
        for j in range(48):
            P.op("pe", lambda e, j=j: e.transpose(out=psum[0][:, j * 6:(j + 1) * 6], in_=mod_sb[0:6, j * 128:(j + 1) * 128],
                                                  identity=identF[0:6, 0:6]), reads=[b_mod, b_const], writes=[b_ps[0]])
        P.op("dve", lambda e, l=l: e.tensor_copy(out=modT[l][:].rearrange("p k b -> p (k b)"), in_=psum[0][:, 0:288]),
             reads=[b_ps[0]], writes=[b_modT[l]])
        P.dma(lambda e, l=l: e.dma_start(out=g16[:], in_=I["norm_g"][l].rearrange("(c p) -> c p", p=128)), writes=[b_g16])
        P.op("pe", lambda e: e.transpose(out=psum[1][:, 0:16], in_=g16[0:16, :], identity=identF[0:16, 0:16]),
             reads=[b_g16, b_const], writes=[b_ps[1]])
        P.op("dve", lambda e: e.tensor_copy(out=gT[:], in_=psum[1][:, 0:16]), reads=[b_ps[1]], writes=[b_gT])
        P.op("dve", lambda e, l=l: e.scalar_tensor_tensor(out=Acol[l][:], in0=modT[l][:, 16:32, :], scalar=1.0,
                                                          in1=gT[:].unsqueeze(2).to_broadcast([128, 16, 6]),
                                                          op0=ALU.add, op1=ALU.mult),
             reads=[b_modT[l], b_gT], writes=[b_A[l]])

    for b in b_hT:
        b.w = b_mod.w
        b.r = dict(b_mod.r)
        for kk_, vv_ in list(b_bmod.r.items()) + [b_bmod.w] + list(b_c6.r.items()) + [b_c6.w]:
            if b.r.get(kk_, 0) < vv_:
                b.r[kk_] = vv_

    b_res = [Buf() for _ in range(16)]
    def mm(out, lhsT, rhs, start, stop, reads, writes):
        P.op("pe", lambda e: e.matmul(out, lhsT=lhsT, rhs=rhs, start=start, stop=stop), reads, writes)

    def tr(out, in_, ident, reads, writes):
        P.op("pe", lambda e: e.transpose(out=out, in_=in_, identity=ident), reads + [b_const], writes)

    def act(out, in_, func, reads, writes, **kw):
        P.op("act", lambda e: e.activation(out=out, in_=in_, func=func, **kw), reads, writes)

    def tt(eng, out, in0, in1, op, reads, writes):
        P.op(eng, lambda e: e.tensor_tensor(out=out, in0=in0, in1=in1, op=op), reads, writes)

    def ts(eng, out, in0, s1, s2, op0, op1, reads, writes):
        if s2 is None:
            P.op(eng, lambda e: e.tensor_scalar(out=out, in0=in0, scalar1=s1, scalar2=None, op0=op0), reads, writes)
        else:
            P.op(eng, lambda e: e.tensor_scalar(out=out, in0=in0, scalar1=s1, scalar2=s2, op0=op0, op1=op1), reads, writes)

    def stt(out, in0, scalar, in1, op0, op1, reads, writes):
        P.op("dve", lambda e: e.scalar_tensor_tensor(out=out, in0=in0, scalar=scalar, in1=in1, op0=op0, op1=op1), reads, writes)

    def cp(eng, out, in_, reads, writes):
        if eng == "act":
            act(out, in_, AF.Copy, reads, writes)
        else:
            P.op(eng, lambda e: e.tensor_copy(out=out, in_=in_), reads, writes)

    def dma(out, in_, reads=(), writes=()):
        P.dma(lambda e: e.dma_start(out=out, in_=in_), reads, writes)

    MUL, ADD, SUB = ALU.mult, ALU.add, ALU.subtract

    WoD = dram("WoD", [D, D], BF16); b_WoD = Buf()
    WsD = dram("WsD", [D, 2 * D], BF16); b_WsD = Buf()
    WgD = dram("WgD", [D, D], BF16); b_WgD = Buf()
    W2D = dram("W2D", [D, D], BF16); b_W2D = Buf()
    conv_weight(I["w_out_att"], WoD, b_WoD, D)
    conv_weight(I["w_in_ssm"], WsD, b_WsD, 2 * D)
    conv_weight(I["w_glu"], WgD, b_WgD, D)
    conv_weight(I["w_out_ssm"], W2D, b_W2D, D)
    NTOK = 4224
    mixD = dram("mixD", [NTOK, D], BF16); b_mixD = Buf()
    x1D = dram("x1D", [NTOK, D], F32); b_x1D = Buf()
    gD = dram("gD", [NTOK, D], BF16); b_gD = Buf()
    szD = dram("szD", [NTOK, D], BF16); b_szD = Buf()
    mD = dram("mD", [NTOK, D], BF16); b_mD = Buf()

    triB = P.sb("triB", [128, 128], BF16)
    triF = P.sb("triF", [128, 128], F32)
    onesF = P.sb("onesF", [128, 128], F32)
    E0 = P.sb("E0", [128, 128], F32)
    for t_ in (triB, triF):
        P.op("pool", lambda e, t_=t_: e.memset(t_[:], 1.0), writes=[b_const])
        P.op("pool", lambda e, t_=t_: e.affine_select(out=t_[:], in_=t_[:], pattern=[[1, 128]], compare_op=ALU.is_ge,
                                                       fill=0.0, base=0, channel_multiplier=-1), reads=[b_const], writes=[b_const])
    P.op("pool", lambda e: e.memset(onesF[:], 1.0), writes=[b_const])
    P.op("pool", lambda e: e.memset(E0[:], 1.0), writes=[b_const])
    P.op("pool", lambda e: e.affine_select(out=E0[:], in_=E0[:], pattern=[[0, 128]], compare_op=ALU.is_equal,
                                           fill=0.0, base=0, channel_multiplier=1), reads=[b_const], writes=[b_const])

    ss = P.sb("ss", [128, 2], F32)
    b_ss = Buf()
    psT = [P.ps("psT%d" % i, [128, 8, 128], BF16) for i in range(2)]
    b_psT = [Buf(True), Buf(True)]
    ev_rr = [0]

    def load_norm_tile(src_rows, s, gain_bc=None):
        dma(stgF[s][:], src_rows, reads=[b_x1D], writes=[b_stgF[s]])
        act(stgB[s][:], stgF[s][:], AF.Square, [b_stgF[s]], [b_stgB[s], b_ss], accum_out=ss[:, 0:1])
        act(ss[:, 1:2], ss[:, 0:1], AF.Sqrt, [b_ss], [b_ss], scale=1.0 / D, bias=EPS)
        P.op("dve", lambda e: e.reciprocal(out=ss[:, 1:2], in_=ss[:, 1:2]), [b_ss], [b_ss])

    def transposes_to(dst_fn, src_tile, b_src, evac):
        for half in range(2):
            for kk in range(8):
                k = half * 8 + kk
                tr(psT[half][:, kk, :], src_tile[:, k * 128:(k + 1) * 128], identB[:], [b_src], [b_psT[half]])
            evac(half)

    def make_hT_tile(src_rows, l, tslot, bcols):
        s = tslot % 2
        load_norm_tile(src_rows, s)
        act(stgB[s][:], stgF[s][:], AF.Copy, [b_stgF[s], b_ss], [b_stgB[s]], scale=ss[:, 1:2])

        def evac(half):
            for kk in range(8):
                k = half * 8 + kk
                for (r0, nr, bc) in bcols:
                    out = hT[:, k, tslot * 128 + r0: tslot * 128 + r0 + nr]
                    in_ = psT[half][:, kk, r0:r0 + nr]
                    if ev_rr[0] % 2 == 0:
                        act(out, in_, AF.Identity, [b_psT[half], b_A[l], b_modT[l]], [b_hT[tslot]] + b_res,
                            scale=Acol[l][:, k, bc:bc + 1], bias=modT[l][:, k, bc:bc + 1])
                    else:
                        ts("dve", out, in_, Acol[l][:, k, bc:bc + 1], modT[l][:, k, bc:bc + 1], MUL, ADD,
                           [b_psT[half], b_A[l], b_modT[l]], [b_hT[tslot]] + b_res)
                    ev_rr[0] += 1
        transposes_to(None, stgB[s], b_stgB[s], evac)

    ph_att = P.phase()
    ph_att.__enter__()
    SLOPES = [2.0 ** (-(h + 1)) for h in range(8)]
    I32 = mybir.dt.int32
    iot = P.sb("iot", [128, 128], I32)
    b_iot = Buf()
    qms = P.sb("qms", [128, 128], F32)
    qrow = P.sb("qrow", [128, 128], F32)
    Bd = P.sb("Bd", [128, 8, 128], F32)
    Bds = P.sb("Bds", [128, 8, 32], F32)
    boffP = P.sb("boffP", [128, 8, 16], F32)
    bpast = P.sb("bpast", [128, 8, 32], F32)
    tmpc = P.sb("tmpc", [128, 128], F32)
    b_tmpc = Buf()
    P.op("pool", lambda e: e.iota(iot[:], pattern=[[1, 128]], base=0, channel_multiplier=-1), writes=[b_iot])
    cp("dve", qms[:], iot[:], [b_iot], [b_const])
    P.op("pool", lambda e: e.iota(iot[:], pattern=[[1, 128]], base=-127, channel_multiplier=0), [], [b_iot])
    cp("dve", qrow[:], iot[:], [b_iot], [b_const])
    act(tmpc[:], qms[:], AF.Abs, [b_const], [b_tmpc])
    tt("dve", tmpc[:], qrow[:], tmpc[:], SUB, [b_const, b_tmpc], [b_tmpc])
    for h in range(8):
        ts("dve", Bd[:, h, :], tmpc[:], SLOPES[h], None, MUL, None, [b_tmpc], [b_const])
        ts("dve", Bds[0:32, h, :], tmpc[0:32, 0:32], 96.0, SLOPES[h], ADD, MUL, [b_tmpc], [b_const])
    P.op("dve", lambda e: e.memset(Bd[64:128, :, 0:64], -1e30), [b_const], [b_const])
    P.op("pool", lambda e: e.iota(iot[:, 0:16], pattern=[[-128, 16]], base=-127, channel_multiplier=1), [b_iot], [b_iot])
    cp("dve", tmpc[:, 0:16], iot[:, 0:16], [b_iot], [b_tmpc])
    for h in range(8):
        ts("dve", boffP[:, h, :], tmpc[:, 0:16], SLOPES[h], None, MUL, None, [b_tmpc], [b_const])
    P.op("pool", lambda e: e.iota(iot[:, 0:32], pattern=[[128, 32]], base=-4127, channel_multiplier=1), [b_iot], [b_iot])
    cp("dve", tmpc[:, 0:32], iot[:, 0:32], [b_iot], [b_tmpc])
    for h in range(8):
        ts("dve", bpast[:, h, :], tmpc[:, 0:32], SLOPES[h], None, MUL, None, [b_tmpc], [b_const])

    LAMBDA_INIT = 0.2
    lqk = P.sb("lqk", [128, 4, 64], F32)
    b_lqk = Buf()
    lam = P.sb("lam", [128, 4], F32)
    b_lam = Buf()
    gsub = P.sb("gsub", [128, 128], F32)
    for i_, nm in enumerate(["lq1", "lk1", "lq2", "lk2"]):
        dma(lqk[:, i_, :], I[nm].to_broadcast([128, 64]), writes=[b_lqk])
    dma(gsub[:], I["subln_g"].to_broadcast([128, 128]), writes=[b_const])
    ts("dve", gsub[:], gsub[:], 1.0 - LAMBDA_INIT, None, MUL, None, [b_const], [b_const])
    for m_ in range(2):
        tt("dve", lqk[:, 2 * m_, :], lqk[:, 2 * m_, :], lqk[:, 2 * m_ + 1, :], MUL, [b_lqk], [b_lqk])
        P.op("dve", lambda e, m_=m_: e.tensor_reduce(out=lam[:, 3:4], in_=lqk[:, 2 * m_, :], axis=mybir.AxisListType.X, op=ADD),
             [b_lqk], [b_lam])
        act(lam[:, m_:m_ + 1], lam[:, 3:4], AF.Exp, [b_lam], [b_lam])
    tt("dve", lam[:, 2:3], lam[:, 1:2], lam[:, 0:1], SUB, [b_lam], [b_lam])
    ts("dve", lam[:, 2:3], lam[:, 2:3], -LAMBDA_INIT, None, ADD, None, [b_lam], [b_lam])

    slab = P.sb("slab", [128, 16, 512], BF16)
    b_slab = Buf()
    fslab = P.sb("fslab", [128, 16, 8], BF16)
    b_fslab = Buf()
    kvst = [P.sb("kvst%d" % i, [128, 256], F32) for i in range(2)]
    b_kvst = [Buf(), Buf()]
    bf_bc = P.sb("bf_bc", [128, 8], F32)
    b_bf = Buf()
    logf = P.sb("logf", [128, 33, 8], F32)
    b_logf = Buf()
    lf_tmp = P.sb("lf_tmp", [128, 8], F32)
    b_lft = Buf()
    lfs = P.sb("lfs", [128, 8], F32)
    b_lfs = Buf()
    Fall = P.sb("Fall", [128, 33, 8], F32)
    b_Fall = Buf()
    ftot = P.sb("ftot", [128, 8], F32)
    b_ftot = Buf()
    biasF = P.sb("biasF", [128, 400, 8], F32)
    b_biasF = Buf()
    QKtm = P.sb("QKtm", [128, 16, 256], BF16)
    b_QKtm = Buf()
    QT = P.sb("QT", [128, 2048], BF16)
    b_QT = Buf()
    KT = P.sb("KT", [128, 4224], BF16)
    b_KT = Buf()
    KTn = P.sb("KTn", [128, 128], BF16)
    b_KTn = Buf()
    Vaug = P.sb("Vaug", [128, 33, 130], BF16)
    b_V = Buf()
    SZ = P.sb("SZ", [128, 16, 128], BF16)
    b_SZ = Buf()
    SZs = P.sb("SZs", [128, 128], BF16)
    b_SZs = Buf()
    PTb = [P.sb("PT%d" % i, [128, 128], BF16) for i in range(4)]
    b_PT = [Buf() for _ in range(4)]
    dtmp = [P.sb("dtmp%d" % i, [128, 128], F32) for i in range(2)]
    b_dtmp = [Buf(), Buf()]
    otmp = P.sb("otmp", [128, 132], F32)
    b_otmp = Buf()
    rden = P.sb("rden", [128, 4], F32)
    b_rden = Buf()
    mixt = [P.sb("mixt%d" % i, [128, 128], BF16) for i in range(2)]
    b_mixt = [Buf(), Buf()]
    cst = [P.sb("cst0", [128, 8, 128], F32)] * 2
    b_cst = [Buf()] * 2
    cKb = P.sb("cKb", [128, 8, 128], BF16)
    b_cKb = Buf()

    P.op("pool", lambda e: e.memset(Vaug[:, :, 128:129], 1.0), [], [b_V])
    dma(bf_bc[:], I["b_forget"].to_broadcast([128, 8]), writes=[b_bf])
    dma(fslab[:], WiaD[:, 3072:3080].rearrange("(k p) n -> p k n", p=128), reads=[b_WiaD], writes=[b_fslab])

    SLAB_COLS = []
    for h in range(8):
        SLAB_COLS.append([h * 128, 1024 + h * 128, 2048 + h * 128, 3080 + h * 128])
    for h in range(8):
        SLAB_COLS.append([4104 + h * 128, 5128 + h * 128, 6152 + h * 128, 7176 + h * 128])

    rr = {"ps": 0, "pt": 0, "kv": 0, "mix": 0, "cst": 0, "ev": 0}

    def f_cumsum(tile_idx, nr):
        mm(psum[4][0:nr, 0:8], triF[0:nr, 0:nr], logf[0:nr, tile_idx, :], True, True, [b_logf, b_const], [b_ps[4]])
        mm(psum[4][:, 8:16], onesF[0:nr, :], logf[0:nr, tile_idx, :], True, True, [b_logf, b_const], [b_ps[4]])
        tt("dve", Fall[0:nr, tile_idx, :], psum[4][0:nr, 0:8], ftot[0:nr, :], ADD, [b_ps[4], b_ftot], [b_Fall])
        tt("dve", ftot[:], psum[4][:, 8:16], ftot[:], ADD, [b_ps[4], b_ftot], [b_ftot])

    def f_bias(base, first_tile, nr_first, nj):
        mm(psum[5][:, 0:8], E0[0:nr_first, :], Fall[0:nr_first, first_tile, :], True, True, [b_Fall, b_const], [b_ps[5]])
        cp("dve", lf_tmp[:], psum[5][:, 0:8], [b_ps[5]], [b_lft])
        tt("dve", biasF[:, base:base + nj, :], lf_tmp[:].unsqueeze(1).to_broadcast([128, nj, 8]), Fall[:, 0:nj, :], SUB,
           [b_lft, b_Fall], [b_biasF])

    def attention(isfox, h, qtiles):
        scale = (128.0 ** -0.5) if isfox else 0.125
        nmap = 1 if isfox else 2
        for qi, q in enumerate(qtiles):
            nq = q["nq"]
            if isfox:
                pso = [psum[2 + (qi % 2)]]
                b_pso = [b_ps[2 + (qi % 2)]]
            else:
                pso = [psum[2 + 2 * (qi % 2)], psum[3 + 2 * (qi % 2)]]
                b_pso = [b_ps[2 + 2 * (qi % 2)], b_ps[3 + 2 * (qi % 2)]]
            nk = len(q["keys"])
            for ki, (nr, KTap, vidx, kind, bidx) in enumerate(q["keys"]):
                for m in range(nmap):
                    pb = rr["ps"] % 2
                    rr["ps"] += 1
                    if isfox:
                        mm(psum[pb][0:nr, 0:nq], KTap, q["QT"], True, True, [b_KT, b_QT], [b_ps[pb]])
                    else:
                        mm(psum[pb][0:nr, 0:nq], KTap[64 * m:64 * m + 64, :], q["QT"][64 * m:64 * m + 64, :], True, True,
                           [b_KT, b_QT], [b_ps[pb]])
                    pi = rr["pt"] % 4
                    rr["pt"] += 1
                    PT = PTb[pi]
                    if isfox:
                        act(PT[0:nr, 0:nq], psum[pb][0:nr, 0:nq], AF.Exp, [b_ps[pb], b_biasF], [b_PT[pi]],
                            scale=scale, bias=biasF[0:nr, q["qt"] + bidx, h:h + 1])
                        if kind == "diag":
                            tt("pool", PT[0:nr, 0:nq], PT[0:nr, 0:nq], triB[0:nr, 0:nq], MUL, [b_PT[pi], b_const], [b_PT[pi]])
                    else:
                        if kind == "diag":
                            di = rr["pt"] % 2
                            bd = Bd[:, h, :] if nq == 128 else Bds[0:32, h, :]
                            stt(dtmp[di][0:nr, 0:nq], psum[pb][0:nr, 0:nq], scale, bd, MUL, ADD, [b_ps[pb], b_const], [b_dtmp[di]])
                            act(PT[0:nr, 0:nq], dtmp[di][0:nr, 0:nq], AF.Exp, [b_dtmp[di]], [b_PT[pi]])
                        else:
                            btab = boffP if kind == "off" else bpast
                            act(PT[0:nr, 0:nq], psum[pb][0:nr, 0:nq], AF.Exp, [b_ps[pb], b_const], [b_PT[pi]],
                                scale=scale, bias=btab[0:nr, h, bidx:bidx + 1])
                    mm(pso[m][0:nq, 0:129], PT[0:nr, 0:nq], Vaug[0:nr, vidx, 0:129], ki == 0, ki == nk - 1,
                       [b_PT[pi], b_V], [b_pso[m]])
            mi = rr["mix"] % 2
            rr["mix"] += 1
            if isfox:
                P.op("dve", lambda e, pso=pso, nq=nq: e.reciprocal(out=rden[0:nq, 0:1], in_=pso[0][0:nq, 128:129]),
                     [b_pso[0]], [b_rden])
                stt(mixt[mi][0:nq, :], pso[0][0:nq, 0:128], rden[0:nq, 0:1], q["SZ"], MUL, MUL,
                    [b_pso[0], b_rden, q["b_SZ"]], [b_mixt[mi]])
            else:
                P.op("dve", lambda e, pso=pso, nq=nq: e.reciprocal(out=rden[0:nq, 0:1], in_=pso[0][0:nq, 128:129]),
                     [b_pso[0]], [b_rden])
                P.op("dve", lambda e, pso=pso, nq=nq: e.reciprocal(out=rden[0:nq, 1:2], in_=pso[1][0:nq, 128:129]),
                     [b_pso[1]], [b_rden])
                tt("dve", rden[0:nq, 1:2], rden[0:nq, 1:2], lam[0:nq, 2:3], MUL, [b_rden, b_lam], [b_rden])
                ts("dve", otmp[0:nq, 0:128], pso[0][0:nq, 0:128], rden[0:nq, 0:1], None, MUL, None, [b_pso[0], b_rden], [b_otmp])
                stt(otmp[0:nq, 0:128], pso[1][0:nq, 0:128], rden[0:nq, 1:2], otmp[0:nq, 0:128], MUL, ADD,
                    [b_pso[1], b_rden, b_otmp], [b_otmp])
                act(dtmp[0][0:nq, :], otmp[0:nq, 0:128], AF.Square, [b_otmp], [b_dtmp[0], b_rden], accum_out=rden[0:nq, 2:3])
                act(rden[0:nq, 3:4], rden[0:nq, 2:3], AF.Sqrt, [b_rden], [b_rden], scale=1.0 / 128, bias=EPS)
                P.op("dve", lambda e, nq=nq: e.reciprocal(out=rden[0:nq, 3:4], in_=rden[0:nq, 3:4]), [b_rden], [b_rden])
                stt(otmp[0:nq, 0:128], otmp[0:nq, 0:128], rden[0:nq, 3:4], gsub[0:nq, :], MUL, MUL,
                    [b_otmp, b_rden, b_const], [b_otmp])
                tt("dve", mixt[mi][0:nq, :], otmp[0:nq, 0:128], q["SZ"], MUL, [b_otmp, q["b_SZ"]], [b_mixt[mi]])
            dma(q["dst"], mixt[mi][0:nq, :], reads=[b_mixt[mi]], writes=[b_mixD])

    if STOP == 'c':
        return finish()
    units = [("p", 0), ("p", 1), ("s", 0)]
    for (kind, q) in units:
        ntile = 16 if kind == "p" else 1
        for t in range(ntile):
            if kind == "p":
                make_hT_tile(I["xp"][(q * 16 + t) * 128:(q * 16 + t + 1) * 128, :], 0, t, [(0, 128, q)])
            else:
                make_hT_tile(I["xs"], 0, 0, [(32 * j, 32, 2 + j) for j in range(4)])
        row0 = q * 2048
        for t in range(ntile):
            pb = 4 + (t % 2)
            for k in range(16):
                mm(psum[pb][:, 0:8], hT[:, k, t * 128:(t + 1) * 128], fslab[:, k, :], k == 0, k == 15, [b_hT[t], b_fslab], [b_ps[pb]])
            tt("dve", lf_tmp[:], psum[pb][:, 0:8], bf_bc[:], ADD, [b_ps[pb], b_bf], [b_lft])
            act(lf_tmp[:], lf_tmp[:], AF.Exp, [b_lft], [b_lft], scale=-1.0)
            act(lf_tmp[:], lf_tmp[:], AF.Ln, [b_lft], [b_lft], bias=1.0)
            lslot = t if kind == "p" else 32
            if kind == "p":
                ts("dve", logf[:, t, :], lf_tmp[:], -1.0, None, MUL, None, [b_lft], [b_logf])
                dma(O["flp"][row0 + t * 128: row0 + (t + 1) * 128, :], logf[:, t, :], reads=[b_logf])
            else:
                ts("dve", lfs[:], lf_tmp[:], -1.0, None, MUL, None, [b_lft], [b_lfs])
                dma(O["fls"], lfs[:], reads=[b_lfs])
        if kind == "p":
            P.op("dve", lambda e: e.memset(ftot[:], 0.0), [b_ftot], [b_ftot])
            for t in range(16):
                f_cumsum(t, 128)
            for t in range(16):
                f_bias(t * 16, t, 128, t + 1)
        else:
            for j in range(4):
                dma(logf[:, 0:32, :], I["cfl"][j].rearrange("(t p) h -> p t h", p=128), writes=[b_logf])
                cp("dve", logf[0:32, 32, :], lfs[32 * j:32 * j + 32, :], [b_lfs], [b_logf])
                P.op("dve", lambda e: e.memset(ftot[:], 0.0), [b_ftot], [b_ftot])
                for t in range(32):
                    f_cumsum(t, 128)
                f_cumsum(32, 32)
                f_bias(256 + 33 * j, 32, 32, 33)
        if STOP == 'u0a':
            return finish()
        for si in range(16):
            isfox = si < 8
            h = si % 8
            if STOP == 'u0c' and si == 1:
                return finish()
            if STOP == 'u0d' and si == 9:
                return finish()
            for j_, c0 in enumerate(SLAB_COLS[si]):
                dma(slab[:, :, j_ * 128:(j_ + 1) * 128], WiaD[:, c0:c0 + 128].rearrange("(k p) n -> p k n", p=128),
                    reads=[b_WiaD], writes=[b_slab])
            for t in range(ntile):
                pb = rr["ps"] % 2
                rr["ps"] += 1
                for k in range(16):
                    mm(psum[pb][:], hT[:, k, t * 128:(t + 1) * 128], slab[:, k, :], k == 0, k == 15, [b_hT[t], b_slab], [b_ps[pb]])
                ks = rr["kv"] % 2
                rr["kv"] += 1
                act(kvst[ks][:], psum[pb][:, 128:384], AF.Copy, [b_ps[pb]], [b_kvst[ks]])
                if kind == "p":
                    rows = slice(row0 + t * 128, row0 + (t + 1) * 128)
                    kd = O["fkp" if isfox else "dkp"][rows, h * 128:(h + 1) * 128]
                    vd = O["fvp" if isfox else "dvp"][rows, h * 128:(h + 1) * 128]
                else:
                    kd = O["fks" if isfox else "dks"][:, h * 128:(h + 1) * 128]
                    vd = O["fvs" if isfox else "dvs"][:, h * 128:(h + 1) * 128]
                dma(kd, kvst[ks][:, 0:128], reads=[b_kvst[ks]])
                dma(vd, kvst[ks][:, 128:256], reads=[b_kvst[ks]])
                KSKIP = os.environ.get('KSKIP', '')
                if 'q' not in KSKIP:
                    cp("dve", QKtm[:, t, 0:128], psum[pb][:, 0:128], [b_ps[pb]], [b_QKtm])
                    cp("pool", QKtm[:, t, 128:256], kvst[ks][:, 0:128], [b_kvst[ks]], [b_QKtm])
                if 's' not in KSKIP:
                    act(SZ[:, t, :], psum[pb][:, 384:512], AF.Silu, [b_ps[pb]], [b_SZ])
                if kind == "p" and 'v' not in KSKIP:
                    cp("pool", Vaug[:, t, 0:128], kvst[ks][:, 128:256], [b_kvst[ks]], [b_V])
            if STOP == 'u0e':
                return finish()
            if kind == "p":
                for part, dst, b_dst in ((0, QT, b_QT), (1, KT, b_KT)):
                    for half in range(2):
                        for kk in range(8):
                            t = half * 8 + kk
                            tr(psT[half][:, kk, :], QKtm[:, t, part * 128:(part + 1) * 128], identB[:], [b_QKtm], [b_psT[half]])
                        eng = "act" if (rr["ev"] % 2 == 0) else "dve"
                        rr["ev"] += 1
                        cp(eng, dst[:, half * 1024:(half + 1) * 1024], psT[half][:].rearrange("p a b -> p (a b)"),
                           [b_psT[half]], [b_dst])
                qtiles = []
                for qt in range(16):
                    keys = []
                    for j in range(qt + 1):
                        if isfox:
                            keys.append((128, KT[:, j * 128:(j + 1) * 128], j, "diag" if j == qt else "off", j))
                        else:
                            keys.append((128, KT[:, j * 128:(j + 1) * 128], j, "diag" if j == qt else "off", qt - j))
                    qtiles.append(dict(nq=128, QT=QT[:, qt * 128:(qt + 1) * 128], SZ=SZ[:, qt, :], b_SZ=b_SZ, qt=qt * 16,
                                       dst=mixD[row0 + qt * 128: row0 + (qt + 1) * 128, si * 128:(si + 1) * 128], keys=keys))
                if STOP == 'u0b':
                    return finish()
                attention(isfox, h, qtiles)
            else:
                tr(psT[0][:, 0, :], QKtm[:, 0, 0:128], identB[:], [b_QKtm], [b_psT[0]])
                tr(psT[0][:, 1, :], QKtm[:, 0, 128:256], identB[:], [b_QKtm], [b_psT[0]])
                cp("dve", QT[:, 0:128], psT[0][:, 0, :], [b_psT[0]], [b_QT])
                cp("act", KTn[:], psT[0][:, 1, :], [b_psT[0]], [b_KTn])
                ks_s = ks
                for j in range(4):
                    if isfox and si == 0:
                        pass
                    ck = I["cfk" if isfox else "cdk"]
                    cv = I["cfv" if isfox else "cdv"]
                    for pc in range(4):
                        c1 = rr["cst"] % 2
                        rr["cst"] += 1
                        dma(cst[c1][:], ck[j, pc * 1024:(pc + 1) * 1024, h * 128:(h + 1) * 128].rearrange("(t p) d -> p t d", p=128),
                            writes=[b_cst[c1]])
                        cp("pool", cKb[:], cst[c1][:], [b_cst[c1]], [b_cKb])
                        for kk in range(8):
                            tr(psT[1][:, kk, :], cKb[:, kk, :], identB[:], [b_cKb], [b_psT[1]])
                        cp("act" if pc % 2 == 0 else "dve", KT[:, pc * 1024:(pc + 1) * 1024],
                           psT[1][:].rearrange("p a b -> p (a b)"), [b_psT[1]], [b_KT])
                        c2 = rr["cst"] % 2
                        rr["cst"] += 1
                        dma(cst[c2][:], cv[j, pc * 1024:(pc + 1) * 1024, h * 128:(h + 1) * 128].rearrange("(t p) d -> p t d", p=128),
                            writes=[b_cst[c2]])
                        cp("pool", Vaug[:, pc * 8:(pc + 1) * 8, 0:128], cst[c2][:], [b_cst[c2]], [b_V])
                    cp("dve", KT[:, 4096:4128], KTn[:, 32 * j:32 * j + 32], [b_KTn], [b_KT])
                    cp("dve", Vaug[0:32, 32, 0:128], kvst[ks_s][32 * j:32 * j + 32, 128:256], [b_kvst[ks_s]], [b_V])
                    cp("dve", SZs[0:32, :], SZ[32 * j:32 * j + 32, 0, :], [b_SZ], [b_SZs])
                    keys = []
                    for jt in range(32):
                        keys.append((128, KT[:, jt * 128:(jt + 1) * 128], jt, "off" if isfox else "past", jt))
                    keys.append((32, KT[:, 4096:4128], 32, "diag", 32))
                    qtiles = [dict(nq=32, QT=QT[:, 32 * j:32 * j + 32], SZ=SZs[0:32, :], b_SZ=b_SZs, qt=256 + 33 * j,
                                   dst=mixD[4096 + 32 * j: 4096 + 32 * j + 32, si * 128:(si + 1) * 128], keys=keys)]
                    attention(isfox, h, qtiles)
        if STOP == 'u0':
            return finish()

    if STOP == 'att':
        return finish()
    ph_att.__exit__(None, None, None)

    gate_bc = P.sb("gate_bc", [128, D], F32)
    b_gate = Buf()
    x1t = P.sb("x1t", [128, D], F32)
    b_x1t = Buf()
    hTt = P.sb("hTt", [128, 16, 128], BF16)
    b_hTt = Buf()
    obf = P.sb("obf", [128, D], BF16)
    b_obf = Buf()

    def load_resident(WD, b_WD, c0=0):
        for k in range(16):
            dma(hT[:, k, :], WD[k * 128:(k + 1) * 128, c0:c0 + D], reads=[b_WD], writes=[b_res[k]] + b_hT)

    def load_gate(l, tile_idx):
        if tile_idx < 32:
            if tile_idx % 16 == 0:
                dma(gate_bc[:], modD[l, tile_idx // 16, 2 * D:3 * D].unsqueeze(0).to_broadcast([128, D]), reads=[b_modD], writes=[b_gate])
        else:
            for j in range(4):
                dma(gate_bc[32 * j:32 * j + 32, :], modD[l, 2 + j, 2 * D:3 * D].unsqueeze(0).to_broadcast([32, D]),
                    reads=[b_modD], writes=[b_gate])

    def tile_rows(ap, ti):
        return ap[ti * 128:(ti + 1) * 128, :]

    def xsrc(ti):
        return I["xp"][ti * 128:(ti + 1) * 128, :] if ti < 32 else I["xs"]

    def dense_from_bf16_tile(srcD, b_srcD, ti, consume, pre=None):
        s = ti % 2
        dma(stgB[s][:], tile_rows(srcD, ti), reads=[b_srcD], writes=[b_stgB[s]])
        if pre is not None:
            pre(s)

        def evac(half):
            eng = "act" if (rr["ev"] % 2 == 0) else "dve"
            rr["ev"] += 1
            cp(eng, mT[:, half * 8:(half + 1) * 8, :], psT[half][:], [b_psT[half]], [b_mT])
        transposes_to(None, stgB[s], b_stgB[s], evac)
        for nb in range(4):
            pb = nb
            for k in range(16):
                mm(psum[pb][:], mT[:, k, :], hT[:, k, nb * 512:(nb + 1) * 512], k == 0, k == 15, [b_mT, b_res[k]], [b_ps[pb]])
            consume(nb, psum[pb], b_ps[pb], s)

    ph_o = P.phase()
    ph_o.__enter__()
    mT = P.sb("mT", [128, 16, 128], BF16)
    b_mT = Buf()
    load_resident(WoD, b_WoD)
    for ti in range(33):
        load_gate(0, ti)

        def pre(s, ti=ti):
            dma(stgF[s][:], xsrc(ti), writes=[b_stgF[s]])

        def consume(nb, ps_, b_p, s):
            cs = slice(nb * 512, (nb + 1) * 512)
            tt("dve", x1t[:, cs], ps_[:], gate_bc[:, cs], MUL, [b_p, b_gate], [b_x1t])
            tt("pool", x1t[:, cs], x1t[:, cs], stgF[s][:, cs], ADD, [b_x1t, b_stgF[s]], [b_x1t])
        dense_from_bf16_tile(mixD, b_mixD, ti, consume, pre)
        dma(tile_rows(x1D, ti), x1t[:], reads=[b_x1t], writes=[b_x1D])

    if STOP == 'o0':
        return finish()
    ph_o.__exit__(None, None, None)
    def make_hTt(ti, l):
        s = ti % 2
        load_norm_tile(tile_rows(x1D, ti), s)
        act(stgB[s][:], stgF[s][:], AF.Copy, [b_stgF[s], b_ss], [b_stgB[s]], scale=ss[:, 1:2])
        bcols = [(0, 128, ti // 16)] if ti < 32 else [(32 * j, 32, 2 + j) for j in range(4)]

        def evac(half):
            for kk in range(8):
                k = half * 8 + kk
                for (r0, nr, bc) in bcols:
                    out = hTt[:, k, r0:r0 + nr]
                    in_ = psT[half][:, kk, r0:r0 + nr]
                    if rr["ev"] % 2 == 0:
                        act(out, in_, AF.Identity, [b_psT[half], b_A[l], b_modT[l]], [b_hTt],
                            scale=Acol[l][:, k, bc:bc + 1], bias=modT[l][:, k, bc:bc + 1])
                    else:
                        ts("dve", out, in_, Acol[l][:, k, bc:bc + 1], modT[l][:, k, bc:bc + 1], MUL, ADD,
                           [b_psT[half], b_A[l], b_modT[l]], [b_hTt])
                    rr["ev"] += 1
        transposes_to(None, stgB[s], b_stgB[s], evac)

    load_resident(WsD, b_WsD, c0=D)
    for ti in range(33):
        make_hTt(ti, 1)
        for nb in range(4):
            for k in range(16):
                mm(psum[nb][:], hTt[:, k, :], hT[:, k, nb * 512:(nb + 1) * 512], k == 0, k == 15, [b_hTt, b_res[k]], [b_ps[nb]])
            act(obf[:, nb * 512:(nb + 1) * 512], psum[nb][:], AF.Silu, [b_ps[nb]], [b_obf])
        dma(tile_rows(szD, ti), obf[:], reads=[b_obf], writes=[b_szD])

    if STOP == 'l1b':
        return finish()
    ph_s5 = P.phase()
    ph_s5.__enter__()
    PI = 3.141592653589793
    BreT = P.sb("BreT", [128, 2, 16, 128], BF16)
    BimT = P.sb("BimT", [128, 2, 16, 128], BF16)
    rmask = P.sb("rmask", [128, 2], F32)
    CreF = P.sb("CreF", [128, 64, 32], F32)
    CimN = P.sb("CimN", [128, 64, 32], F32)
    Dblk = P.sb("Dblk", [128, 4, 16, 32], BF16)
    rmask4 = P.sb("rmask4", [128, 4], F32)
    ident32 = P.sb("ident32", [128, 32], F32)
    dcol = P.sb("dcol", [128, 16], F32)
    ArN = P.sb("ArN", [128, 64, 4], F32)
    AiN = P.sb("AiN", [128, 64, 4], F32)
    b_s5c = Buf()
    uT = P.sb("uT", [128, 16, 256], BF16)
    b_uT = Buf()
    BUf = P.sb("BUf", [128, 4096], F32)
    b_BU = Buf()
    XhFl = P.sb("XhFl", [128, 8448], F32)
    b_Xh = Buf()
    tmps = [P.sb("s5t%d" % i, [128, 256], F32) for i in range(4)]
    b_tmps = [Buf() for _ in range(4)]
    sinit = P.sb("sinit", [128, 2, 4, 64], F32)
    b_sinit = Buf()
    st64 = P.sb("st64", [64, 128], F32)
    b_st64 = Buf()

    P.op("dve", lambda e: e.tensor_reduce(out=rmask[:, 0:1], in_=identF[:, 0:32], axis=mybir.AxisListType.X, op=ADD), [b_const], [b_s5c])
    P.op("dve", lambda e: e.tensor_reduce(out=rmask[:, 1:2], in_=identF[:, 64:96], axis=mybir.AxisListType.X, op=ADD), [b_const], [b_s5c])
    tt("dve", rmask[:, 0:1], rmask[:, 0:1], rmask[:, 1:2], ADD, [b_s5c], [b_s5c])
    ts("dve", rmask[:, 1:2], rmask[:, 0:1], -1.0, 1.0, MUL, ADD, [b_s5c], [b_s5c])
    prm = stgF[1][:, 0:1024].rearrange("p (a b) -> p a b", a=16)
    b_prm = b_stgF[1]

    def PR(i):
        return prm[:, i, :]

    def loadT(src, dst, b_dst):
        dma(stgF[0][0:64, 0:128], src, writes=[b_stgF[0]])
        tr(psum[0][:, 0:64], stgF[0][0:64, 0:128], identF[0:64, 0:64], [b_stgF[0]], [b_ps[0]])
        cp("dve", dst, psum[0][:, 0:64], [b_ps[0]], [b_dst])

    loadT(I["a_re"].rearrange("(gp g2) p -> gp (g2 p)", g2=2), PR(0), b_prm)
    loadT(I["a_im"].rearrange("(gp g2) p -> gp (g2 p)", g2=2), PR(1), b_prm)
    dma(stgF[0][0:64, 128:130], I["log_dt"][0].rearrange("(gp g2) -> gp g2", g2=2), writes=[b_stgF[0]])
    cp("dve", stgF[0][0:64, 256:384].rearrange("p (a b) -> p a b", a=2),
       stgF[0][0:64, 128:130].unsqueeze(2).to_broadcast([64, 2, 64]), [b_stgF[0]], [b_stgF[0]])
    tr(psum[0][:, 0:64], stgF[0][0:64, 256:384], identF[0:64, 0:64], [b_stgF[0]], [b_ps[0]])
    cp("dve", PR(2), psum[0][:, 0:64], [b_ps[0]], [b_prm])
    bp = [b_prm]
    act(PR(2), PR(2), AF.Exp, bp, bp)
    tt("dve", PR(3), PR(2), PR(0), MUL, bp, bp)
    tt("dve", PR(4), PR(2), PR(1), MUL, bp, bp)
    act(PR(3), PR(3), AF.Exp, bp, bp)
    ki = P.sb("ki", [128, 64], mybir.dt.int32)

    def sin_of(dst, shift):
        ts("dve", PR(13), PR(4), shift, None, ADD, None, bp, bp)
        ts("dve", PR(11), PR(13), 1.0 / (2 * PI), None, MUL, None, bp, bp)
        cp("dve", ki[:], PR(11), bp, bp)
        cp("dve", PR(11), ki[:], bp, bp)
        stt(PR(11), PR(11), -2 * PI, PR(13), MUL, ADD, bp, bp)
        ts("dve", PR(12), PR(11), PI, 2 * PI, ALU.is_gt, MUL, bp, bp)
        tt("dve", PR(11), PR(11), PR(12), SUB, bp, bp)
        ts("dve", PR(12), PR(11), -PI, 2 * PI, ALU.is_lt, MUL, bp, bp)
        tt("dve", PR(11), PR(11), PR(12), ADD, bp, bp)
        act(dst, PR(11), AF.Sin, bp, bp)
    sin_of(PR(6), 0.0)
    sin_of(PR(5), 0.5 * PI)
    tt("dve", PR(5), PR(5), PR(3), MUL, bp, bp)
    tt("dve", PR(6), PR(6), PR(3), MUL, bp, bp)
    tt("dve", PR(11), PR(0), PR(0), MUL, bp, bp)
    tt("dve", PR(12), PR(1), PR(1), MUL, bp, bp)
    tt("dve", PR(7), PR(11), PR(12), ADD, bp, bp)
    P.op("dve", lambda e: e.reciprocal(out=PR(7), in_=PR(7)), bp, bp)
    ts("dve", PR(8), PR(5), -1.0, None, ADD, None, bp, bp)
    tt("dve", PR(11), PR(8), PR(0), MUL, bp, bp)
    tt("dve", PR(12), PR(6), PR(1), MUL, bp, bp)
    tt("dve", PR(9), PR(11), PR(12), ADD, bp, bp)
    tt("dve", PR(9), PR(9), PR(7), MUL, bp, bp)
    tt("dve", PR(11), PR(6), PR(0), MUL, bp, bp)
    tt("dve", PR(12), PR(8), PR(1), MUL, bp, bp)
    tt("dve", PR(10), PR(11), PR(12), SUB, bp, bp)
    tt("dve", PR(10), PR(10), PR(7), MUL, bp, bp)
    cp("dve", ArN[:], PR(5).unsqueeze(2).to_broadcast([128, 64, 4]), bp, [b_s5c])
    cp("dve", AiN[:], PR(6).unsqueeze(2).to_broadcast([128, 64, 4]), bp, [b_s5c])

    v3 = lambda ap: ap.rearrange("p (a b) -> p a b", a=64)
    bR = v3(gate_bc[:, 0:1024])
    bI = v3(gate_bc[:, 1024:2048])
    t1v = v3(stgF[0][:, 0:1024])
    t2v = v3(stgF[0][:, 1024:2048])
    bbv = v3(stgF[1][:, 1024:2048])
    for g2 in range(2):
        dma(bR[64 * g2:64 * g2 + 64], I["b_re"].rearrange("(gp g2) p c -> g2 p gp c", g2=2)[g2], writes=[b_gate])
        dma(bI[64 * g2:64 * g2 + 64], I["b_im"].rearrange("(gp g2) p c -> g2 p gp c", g2=2)[g2], writes=[b_gate])
    frb = PR(9).unsqueeze(2).to_broadcast([128, 64, 16])
    fib = PR(10).unsqueeze(2).to_broadcast([128, 64, 16])
    bt = [b_stgF[0]]
    Mfull = x1t[:].rearrange("p (a b c) -> p a b c", a=64, b=2)
    for which, BT in ((0, BreT), (1, BimT)):
        if which == 0:
            tt("dve", t1v, bR, frb, MUL, [b_gate, b_prm], bt)
            tt("dve", t2v, bI, fib, MUL, [b_gate, b_prm], bt)
            tt("dve", bbv, t1v, t2v, SUB, bt, [b_prm])
        else:
            tt("dve", t1v, bI, frb, MUL, [b_gate, b_prm], bt)
            tt("dve", t2v, bR, fib, MUL, [b_gate, b_prm], bt)
            tt("dve", bbv, t1v, t2v, ADD, bt, [b_prm])
        P.op("dve", lambda e: e.memset(x1t[:], 0.0), [b_x1t], [b_x1t])
        cp("dve", Mfull[0:64, :, 0, :], bbv[0:64], [b_prm], [b_x1t])
        cp("dve", Mfull[64:128, :, 1, :], bbv[64:128], [b_prm], [b_x1t])
        for i4 in range(4):
            for j4 in range(4):
                ct = i4 * 4 + j4
                tr(psum[1][:, j4 * 128:(j4 + 1) * 128], x1t[:, ct * 128:(ct + 1) * 128], identF[:], [b_x1t], [b_ps[1]])
            for jj in range(2):
                ts("dve", BT[:, jj, i4 * 4:(i4 + 1) * 4, :], psum[1][:].rearrange("p (a b) -> p a b", a=4), rmask[:, jj:jj + 1], None,
                   MUL, None, [b_ps[1], b_s5c], [b_s5c])
    P.op("dve", lambda e: e.memset(CreF[:], 0.0), [b_s5c], [b_s5c])
    P.op("dve", lambda e: e.memset(CimN[:], 0.0), [b_s5c], [b_s5c])
    for nm, CT, sgn in (("c_re", CreF, 1.0), ("c_im", CimN, -1.0)):
        for ct in range(16):
            dma(stgF[0][:, 0:64], I[nm].rearrange("g co p -> (g co) p")[ct * 128:(ct + 1) * 128, :], writes=[b_stgF[0]])
            tr(psum[2][0:64, 0:128], stgF[0][:, 0:64], identF[:], [b_stgF[0]], [b_ps[2]])
            pv = psum[2][0:64, 0:128].rearrange("p (a b c) -> p a b c", a=4, b=2)
            ts("dve", CT[0:64, 4 * ct:4 * ct + 4, 0:16], pv[:, :, 0, :], sgn, None, MUL, None, [b_ps[2]], [b_s5c])
            ts("dve", CT[64:128, 4 * ct:4 * ct + 4, 16:32], pv[:, :, 1, :], sgn, None, MUL, None, [b_ps[2]], [b_s5c])
    dma(g16[:], I["ssm_d"][0].rearrange("(c p) -> c p", p=128), writes=[b_g16])
    tr(psum[1][:, 0:16], g16[0:16, :], identF[0:16, 0:16], [b_g16], [b_ps[1]])
    cp("dve", dcol[:], psum[1][:, 0:16], [b_ps[1]], [b_s5c])
    tt("dve", ident32[:], identF[:, 0:32], identF[:, 32:64], ADD, [b_const], [b_s5c])
    tt("dve", ident32[:], ident32[:], identF[:, 64:96], ADD, [b_const, b_s5c], [b_s5c])
    tt("dve", ident32[:], ident32[:], identF[:, 96:128], ADD, [b_const, b_s5c], [b_s5c])
    for v4 in range(4):
        P.op("dve", lambda e, v4=v4: e.tensor_reduce(out=rmask4[:, v4:v4 + 1], in_=identF[:, 32 * v4:32 * v4 + 32],
                                                     axis=mybir.AxisListType.X, op=ADD), [b_const], [b_s5c])
    for ct in range(16):
        for v4 in range(4):
            ts("dve", Dblk[:, v4, ct, :], ident32[:], dcol[:, ct:ct + 1], rmask4[:, v4:v4 + 1], MUL, MUL, [b_s5c], [b_s5c])

    def make_uT(col0):
        for i4 in range(4):
            pb = i4 % 2
            for j4 in range(4):
                ct = i4 * 4 + j4
                for k in range(16):
                    mm(psum[pb][:, j4 * 128:(j4 + 1) * 128], hT[:, k, ct * 128:(ct + 1) * 128], hTt[:, k, :], k == 0, k == 15,
                       [b_res[k], b_hTt], [b_ps[pb]])
            cp("act", uT[:, i4 * 4:(i4 + 1) * 4, col0:col0 + 128], psum[pb][:].rearrange("p (a b) -> p a b", a=4),
               [b_ps[pb]], [b_uT])

    def s5_chunk(nb, gp_lo, ng, ucol, Xh, BU):
        per_bank = 512 // (2 * nb * 16)
        A = ArN[:, gp_lo:gp_lo + ng, 0:nb]
        Bi = AiN[:, gp_lo:gp_lo + ng, 0:nb]
        tv = [t_[:, 0:ng * nb].rearrange("p (a b) -> p a b", a=ng) for t_ in tmps]
        R = 2 * nb * 16
        G = 2 * per_bank
        BU4 = BU.rearrange("p g c q t -> p (g c q t)").rearrange("p (a b r) -> p a b r", b=4, r=R)
        for sub in range(2):
            for gi in range(ng):
                gp = gp_lo + gi
                ct = gp // 4
                kind = (gp % 4) // 2
                h0 = 64 * kind
                jj = gp % 2
                gl = gi % G
                slot = (gl // 4) * 2 + jj
                off = slot * R
                for c, BT in ((0, BreT), (1, BimT)):
                    for q in range(nb):
                        o2 = off + (c * nb + q) * 16
                        u0 = ucol(q) + sub * 16
                        mm(psum[kind][:, o2:o2 + 16], BT[h0:h0 + 64, jj, ct, :], uT[h0:h0 + 64, ct, u0:u0 + 16], True, True,
                           [b_s5c, b_uT], [b_ps[kind]])
                if gl == G - 1:
                    a0 = (gi - G + 1) // 4
                    for kind2 in range(2):
                        cp("act", BU4[:, a0:a0 + G // 4, 2 * kind2:2 * kind2 + 2, :],
                           psum[kind2][:, 0:per_bank * R].rearrange("p (a b r) -> p a b r", b=2, r=R), [b_ps[kind2]], [b_BU])
            for t in range(16):
                tg = sub * 16 + t
                xr = Xh[:, :, 0, :, tg]
                xi = Xh[:, :, 1, :, tg]
                tt("dve", tv[0], A, xr, MUL, [b_s5c, b_Xh], [b_tmps[0]])
                tt("dve", tv[1], Bi, xi, MUL, [b_s5c, b_Xh], [b_tmps[1]])
                tt("dve", tv[2], A, xi, MUL, [b_s5c, b_Xh], [b_tmps[2]])
                tt("dve", tv[3], Bi, xr, MUL, [b_s5c, b_Xh], [b_tmps[3]])
                tt("dve", tv[0], tv[0], tv[1], SUB, [b_tmps[0], b_tmps[1]], [b_tmps[0]])
                tt("dve", tv[2], tv[2], tv[3], ADD, [b_tmps[2], b_tmps[3]], [b_tmps[2]])
                tt("dve", Xh[:, :, 0, :, tg + 1], tv[0], BU[:, :, 0, :, t], ADD, [b_tmps[0], b_BU], [b_Xh])
                tt("dve", Xh[:, :, 1, :, tg + 1], tv[2], BU[:, :, 1, :, t], ADD, [b_tmps[2], b_BU], [b_Xh])

    def s5_cproj(q, gp_lo, ng, ucol, Xh, ytile, b_y, row0):
        for gi in range(ng):
            gp = gp_lo + gi
            ct = gp // 4
            h0 = 64 * ((gp % 4) // 2)
            jj = gp % 2
            bank = 2 + (gp * 32) // 512
            lc = (gp * 32) % 512
            out = psum[bank][0:32, lc:lc + 32]
            mm(out, Xh[:, gi, 0, q, 1:33], CreF[:, gp, :], True, False, [b_Xh, b_s5c], [b_ps[bank]])
            mm(out, Xh[:, gi, 1, q, 1:33], CimN[:, gp, :], False, False, [b_Xh, b_s5c], [b_ps[bank]])
            mm(out, uT[:, ct, ucol(q):ucol(q) + 32], Dblk[:, gp % 4, ct, :], False, True, [b_uT, b_s5c], [b_ps[bank]])
        for bank in range((gp_lo * 32) // 512, ((gp_lo + ng) * 32 + 511) // 512):
            cp("act", ytile[row0:row0 + 32, bank * 512:(bank + 1) * 512], psum[2 + bank][0:32, :], [b_ps[2 + bank]], [b_y])

    def gelu_store(ytile, b_y, ti, sfree):
        tmp = stgF[sfree]
        b_bcB = b_stgF[sfree]
        tt("pool", tmp[:], ytile[:], ytile[:], MUL, [b_y], [b_bcB])
        ts("pool", tmp[:], tmp[:], 0.044715, 1.0, MUL, ADD, [b_bcB], [b_bcB])
        tt("pool", tmp[:], tmp[:], ytile[:], MUL, [b_bcB, b_y], [b_bcB])
        act(tmp[:], tmp[:], AF.Sigmoid, [b_bcB], [b_bcB], scale=1.5957691216)
        tt("pool", obf[:], tmp[:], ytile[:], MUL, [b_bcB, b_y], [b_obf])
        dma(tile_rows(gD, ti), obf[:], reads=[b_obf], writes=[b_gD])

    def store_state(src_ap, dst_ap):
        tr(psum[0][0:64, 0:128], src_ap, identF[:], [b_Xh, b_sinit], [b_ps[0]])
        cp("dve", st64[:], psum[0][0:64, 0:128], [b_ps[0]], [b_st64])
        dma(dst_ap.rearrange("(gp g2) p -> gp (g2 p)", g2=2), st64[:], reads=[b_st64])

    load_resident(WsD, b_WsD, c0=0)
    XhP = XhFl[:].rearrange("p (g c q t) -> p g c q t", g=64, c=2, q=2)
    BUP = BUf[:].rearrange("p (g c q t) -> p g c q t", g=64, c=2, q=2)
    P.op("dve", lambda e: e.memset(XhFl[:], 0.0), [b_Xh], [b_Xh])
    ytiles = [(x1t, b_x1t), (gate_bc, b_gate)]
    for blk in range(16):
        for q in range(2):
            make_hTt(q * 16 + blk, 1)
            make_uT(q * 128)
        for c in range(4):
            ucol = lambda q, c=c: q * 128 + c * 32
            s5_chunk(2, 0, 64, ucol, XhP, BUP)
            for q in range(2):
                s5_cproj(q, 0, 64, ucol, XhP, ytiles[q][0], ytiles[q][1], 32 * c)
            cp("dve", XhP[:, :, :, :, 0], XhP[:, :, :, :, 32], [b_Xh], [b_Xh])
        for q in range(2):
            gelu_store(ytiles[q][0], ytiles[q][1], q * 16 + blk, (blk + 1) % 2)
    for q in range(2):
        store_state(XhP[:, :, 0, q, 0], O["srep"][q])
        store_state(XhP[:, :, 1, q, 0], O["simp"][q])
    XhS = XhFl[:].rearrange("p (g c q t) -> p g c q t", g=32, c=2, q=4)
    BUS = BUf[:].rearrange("p (g c q t) -> p g c q t", g=32, c=2, q=4)
    for q in range(4):
        loadT(I["sre"][q].rearrange("(gp g2) p -> gp (g2 p)", g2=2), sinit[:, 0, q, :], b_sinit)
        loadT(I["sim"][q].rearrange("(gp g2) p -> gp (g2 p)", g2=2), sinit[:, 1, q, :], b_sinit)
    make_hTt(32, 1)
    make_uT(0)
    ucs = lambda q: q * 32
    for hp in range(2):
        for c in range(2):
            cp("dve", XhS[:, :, c, :, 0], sinit[:, c, :, 32 * hp:32 * hp + 32].rearrange("p q g -> p g q"), [b_sinit], [b_Xh])
        s5_chunk(4, 32 * hp, 32, ucs, XhS, BUS)
        for q in range(4):
            s5_cproj(q, 32 * hp, 32, ucs, XhS, x1t, b_x1t, 32 * q)
        for c in range(2):
            cp("dve", sinit[:, c, :, 32 * hp:32 * hp + 32].rearrange("p q g -> p g q"), XhS[:, :, c, :, 32], [b_Xh], [b_sinit])
    gelu_store(x1t, b_x1t, 32, 1)
    for q in range(4):
        store_state(sinit[:, 0, q, :], O["sres"][q])
        store_state(sinit[:, 1, q, :], O["sims"][q])
    if STOP == 's5':
        return finish()
    ph_s5.__exit__(None, None, None)

    bcB = P.sb("bcB", [128, D], F32)
    b_bcB = Buf()
    mT = P.sb("mT2", [128, 16, 128], BF16)
    b_mT = Buf()
    bglu = bcB
    b_bglu = b_bcB
    dma(bglu[:], I["b_glu"].to_broadcast([128, D]), writes=[b_bglu])
    szt = hTt[:].rearrange("p a b -> p (a b)")
    b_szt = b_hTt
    load_resident(WgD, b_WgD)
    for ti in range(33):
        def pre(s, ti=ti):
            dma(szt, tile_rows(szD, ti), reads=[b_szD], writes=[b_szt])

        def consume(nb, ps_, b_p, s):
            cs = slice(nb * 512, (nb + 1) * 512)
            tt("dve", x1t[:, cs], ps_[:], bglu[:, cs], ADD, [b_p, b_bglu], [b_x1t])
            act(x1t[:, cs], x1t[:, cs], AF.Sigmoid, [b_x1t], [b_x1t])
            tt("pool", x1t[:, cs], x1t[:, cs], stgB[s][:, cs], MUL, [b_x1t, b_stgB[s]], [b_x1t])
            tt("pool", obf[:, cs], x1t[:, cs], szt[:, cs], MUL, [b_x1t, b_szt], [b_obf])
        dense_from_bf16_tile(gD, b_gD, ti, consume, pre)
        dma(tile_rows(mD, ti), obf[:], reads=[b_obf], writes=[b_mD])

    gfin = bcB
    b_gfin = b_bcB
    dma(gfin[:], I["final_g"].to_broadcast([128, D]), writes=[b_gfin])
    load_resident(W2D, b_W2D)
    for ti in range(33):
        load_gate(1, ti)

        def pre(s, ti=ti):
            dma(stgF[s][:], tile_rows(x1D, ti), reads=[b_x1D], writes=[b_stgF[s]])

        def consume(nb, ps_, b_p, s):
            cs = slice(nb * 512, (nb + 1) * 512)
            tt("dve", x1t[:, cs], ps_[:], gate_bc[:, cs], MUL, [b_p, b_gate], [b_x1t])
            tt("pool", x1t[:, cs], x1t[:, cs], stgF[s][:, cs], ADD, [b_x1t, b_stgF[s]], [b_x1t])
        dense_from_bf16_tile(mD, b_mD, ti, consume, pre)
        s = ti % 2
        act(stgF[s][:], x1t[:], AF.Square, [b_x1t], [b_stgF[s], b_ss], accum_out=ss[:, 0:1])
        act(ss[:, 1:2], ss[:, 0:1], AF.Sqrt, [b_ss], [b_ss], scale=1.0 / D, bias=EPS)
        P.op("dve", lambda e: e.reciprocal(out=ss[:, 1:2], in_=ss[:, 1:2]), [b_ss], [b_ss])
        stt(x1t[:], x1t[:], ss[:, 1:2], gfin[:], MUL, MUL, [b_x1t, b_ss, b_gfin], [b_x1t])
        dst = O["yp"][ti * 128:(ti + 1) * 128, :] if ti < 32 else O["ys"]
        dma(dst, x1t[:], reads=[b_x1t])


    import os
    if os.environ.get('KDBG'):
        print('CNT', P.cnt, 'ndma', P.ndma, {k: len(v) for k, v in P.streams.items()})
    P.emit()
    return nc


_NC_CACHE = {}


def kernel(x_prompt, x_sample, cache_fox_k, cache_fox_v, cache_fox_logf, cache_diff_k, cache_diff_v,
           state_ssm_re, state_ssm_im, c_prompt, c_sample, norm_g, w_mod, b_mod, w_in_att, b_forget,
           w_out_att, diff_lq1, diff_lk1, diff_lq2, diff_lk2, diff_subln_g, w_in_ssm, ssm_a_re, ssm_a_im,
           ssm_b_re, ssm_b_im, ssm_c_re, ssm_c_im, ssm_d, ssm_log_dt, w_glu, b_glu, w_out_ssm, final_norm_g):
    f = lambda a: np.ascontiguousarray(np.asarray(a, dtype=np.float32))
    if "nc" not in _NC_CACHE:
        _NC_CACHE["nc"] = build()
    nc = _NC_CACHE["nc"]
    shared = {
        "norm_g": f(norm_g), "w_mod": f(w_mod), "b_mod": f(b_mod), "w_in_att": f(w_in_att[0]), "b_forget": f(b_forget),
        "w_out_att": f(w_out_att[0]), "lq1": f(diff_lq1), "lk1": f(diff_lk1), "lq2": f(diff_lq2), "lk2": f(diff_lk2),
        "subln_g": f(diff_subln_g), "w_in_ssm": f(w_in_ssm[0]), "a_re": f(ssm_a_re[0]), "a_im": f(ssm_a_im[0]),
        "b_re": f(ssm_b_re[0]), "b_im": f(ssm_b_im[0]), "c_re": f(ssm_c_re[0]), "c_im": f(ssm_c_im[0]),
        "ssm_d": f(ssm_d), "log_dt": f(ssm_log_dt), "w_glu": f(w_glu[0]), "b_glu": f(b_glu), "w_out_ssm": f(w_out_ssm[0]),
        "final_g": f(final_norm_g).reshape(1, D),
    }
    in_maps = []
    for c in range(NCORES):
        m = dict(shared)
        m["xp"] = f(x_prompt[2 * c:2 * c + 2]).reshape(4096, D)
        m["xs"] = f(x_sample[4 * c:4 * c + 4]).reshape(128, D)
        m["cfk"] = f(cache_fox_k[0, 4 * c:4 * c + 4]).reshape(4, 4096, 1024)
        m["cfv"] = f(cache_fox_v[0, 4 * c:4 * c + 4]).reshape(4, 4096, 1024)
        m["cfl"] = f(cache_fox_logf[0, 4 * c:4 * c + 4]).reshape(4, 4096, 8)
        m["cdk"] = f(cache_diff_k[0, 4 * c:4 * c + 4]).reshape(4, 4096, 1024)
        m["cdv"] = f(cache_diff_v[0, 4 * c:4 * c + 4]).reshape(4, 4096, 1024)
        m["sre"] = f(state_ssm_re[0, 4 * c:4 * c + 4])
        m["sim"] = f(state_ssm_im[0, 4 * c:4 * c + 4])
        m["cc"] = np.concatenate([f(c_prompt[2 * c:2 * c + 2]), f(c_sample[4 * c:4 * c + 4])], 0)
        in_maps.append(m)
    res = run_bass_kernel_spmd(nc, in_maps, core_ids=list(range(NCORES)))
    R = res.results
    cat = lambda n: np.concatenate([np.asarray(R[c][n], dtype=np.float32) for c in range(NCORES)], 0)
    y_prompt = cat("yp").reshape(16, 2048, D)
    y_sample = cat("ys").reshape(32, 32, D)
    fk_p = cat("fkp").reshape(1, 16, 2048, 8, 128)
    fv_p = cat("fvp").reshape(1, 16, 2048, 8, 128)
    fl_p = cat("flp").reshape(1, 16, 2048, 8)
    dk_p = cat("dkp").reshape(1, 16, 2048, 8, 128)
    dv_p = cat("dvp").reshape(1, 16, 2048, 8, 128)
    sre_p = cat("srep").reshape(1, 16, 128, 64)
    sim_p = cat("simp").reshape(1, 16, 128, 64)
    fk_s = cat("fks").reshape(1, 32, 32, 8, 128)
    fv_s = cat("fvs").reshape(1, 32, 32, 8, 128)
    fl_s = cat("fls").reshape(1, 32, 32, 8)
    dk_s = cat("dks").reshape(1, 32, 32, 8, 128)
    dv_s = cat("dvs").reshape(1, 32, 32, 8, 128)
    sre_s = cat("sres").reshape(1, 32, 128, 64)
    sim_s = cat("sims").reshape(1, 32, 128, 64)
    return (y_prompt, y_sample, fk_p, fv_p, fl_p, dk_p, dv_p, sre_p, sim_p,
            fk_s, fv_s, fl_s, dk_s, dv_s, sre_s, sim_s)
```
